# Optimizing a Trainium2 kernel written in Bass

```python
import functools
import jax, jax.numpy as jnp
from jax import lax
import numpy as np

D_MODEL = 1024
BATCH = 8
SEQ = 2048
DEPTH = 2
DEC_BATCH = 128
DEC_SEQ = 4
PAST_LEN = 2048
PAGE_SIZE = 128

N_BRANCH = 4
POOL_GROUPS = 4
POOL_GROUP_W = 64
POOL_W = POOL_GROUPS * POOL_GROUP_W
POOL_WINDOWS = (2, 4, 8, 16)
POOL_STATE = 15
CONF_W = 256
CONF_K = 31
SCONV_W = 256
SCONV_K = 3
N_HEADS = 8
HEAD_DIM = 64
ATT_W = N_HEADS * HEAD_DIM
IDX_HEADS = 4
IDX_DIM = 64
TOPK_MAX = 256
Q_BLOCK = 128
ROPE_THETA = 10000.0
PE_DIM = 256
D_FF = ((8 * D_MODEL + 3 * 256 - 1) // (3 * 256)) * 256
ALPHA = (2 * DEPTH) ** 0.25
BETA = (8 * DEPTH) ** -0.25
LN_EPS = 1e-5
IN_SPLITS = (N_BRANCH * D_MODEL, POOL_W, 2 * CONF_W, 3 * SCONV_W, ATT_W, ATT_W, ATT_W,
             IDX_HEADS * IDX_DIM, IDX_DIM, IDX_HEADS)
IN_COLS = sum(IN_SPLITS)

kernel_name = 'hybrid_pool_conformer_shortconv_dsa_step'


def layer_norm(x, g, b):
    xf = x.astype(jnp.float32)
    mu = jnp.mean(xf, -1, keepdims=True)
    var = jnp.mean(jnp.square(xf - mu), -1, keepdims=True)
    return ((xf - mu) * lax.rsqrt(var + LN_EPS) * g + b).astype(x.dtype)


def split_cols(z):
    pts = np.cumsum(IN_SPLITS)[:-1].tolist()
    return jnp.split(z, pts, axis=-1)


def rope(x, pos):
    half = x.shape[-1] // 2
    inv = ROPE_THETA ** (-jnp.arange(half, dtype=jnp.float32) / half)
    ang = pos.astype(jnp.float32)[:, None] * inv[None, :]
    cos = jnp.cos(ang)[:, None, :]
    sin = jnp.sin(ang)[:, None, :]
    xf = x.astype(jnp.float32)
    x1, x2 = xf[..., :half], xf[..., half:]
    return jnp.concatenate([x1 * cos - x2 * sin, x2 * cos + x1 * sin], -1).astype(x.dtype)


def causal_dwconv(u, prefix, w):
    ext = jnp.concatenate([prefix.astype(u.dtype), u], axis=1)
    y = lax.conv_general_dilated(ext, w[:, None, :].astype(u.dtype), window_strides=(1,), padding='VALID',
                                 dimension_numbers=('NWC', 'WIO', 'NWC'), feature_group_count=u.shape[-1])
    return y, ext[:, ext.shape[1] - (w.shape[0] - 1):]


def pool_mixer(u, prefix, pos, mix_w, scale):
    n, t, _ = u.shape
    p = prefix.shape[1]
    ext = jnp.concatenate([prefix.astype(u.dtype), u], axis=1)
    extf = ext.astype(jnp.float32)
    cs = jnp.concatenate([jnp.zeros((n, 1, POOL_W), jnp.float32), jnp.cumsum(extf, axis=1)], axis=1)
    end = cs[:, p + 1:p + 1 + t]
    means = []
    for g, win in enumerate(POOL_WINDOWS):
        sl = slice(g * POOL_GROUP_W, (g + 1) * POOL_GROUP_W)
        cnt = jnp.minimum(win, pos + 1).astype(jnp.float32)[None, :, None]
        means.append((end[..., sl] - cs[:, p + 1 - win:p + 1 - win + t, sl]) / cnt)
    pooled = (jnp.concatenate(means, -1) - extf[:, p:]).astype(u.dtype)
    mixed = jnp.einsum('ntgc,gcd->ntgd', pooled.reshape(n, t, POOL_GROUPS, POOL_GROUP_W), mix_w)
    return mixed.reshape(n, t, POOL_W) * scale, ext[:, ext.shape[1] - POOL_STATE:]


def index_scores(qi, ki, wi):
    s = jnp.einsum('nqhd,nkd->nqhk', qi.astype(jnp.float32), ki.astype(jnp.float32)) * IDX_DIM ** -0.5
    return jnp.einsum('nqhk,nqh->nqk', jax.nn.relu(s), wi.astype(jnp.float32)) * IDX_HEADS ** -0.5


def sparse_attend(q, kg, vg, valid):
    logits = jnp.einsum('nqhd,nqkhd->nqhk', q, kg).astype(jnp.float32) * HEAD_DIM ** -0.5
    logits = jnp.where(valid[:, :, None, :], logits, -jnp.inf)
    probs = jax.nn.softmax(logits, axis=-1)
    return jnp.einsum('nqhk,nqkhd->nqhd', probs.astype(vg.dtype), vg)


def dsa_prompt(q, k, v, qi, ki, wi):
    n, s, h, d = q.shape
    nb = s // Q_BLOCK
    ktop = min(TOPK_MAX, s // 4)
    key_pos = jnp.arange(s)

    def to_blocks(a):
        return a.reshape((n, nb, Q_BLOCK) + a.shape[2:]).swapaxes(0, 1)

    def block(args):
        qb, qib, wib, posb = args
        sc = index_scores(qib, ki, wib)
        sc = jnp.where(key_pos[None, None, :] <= posb[None, :, None], sc, -jnp.inf)
        vals, idx = lax.top_k(sc, ktop)
        kg = jax.vmap(lambda a, i: a[i])(k, idx)
        vg = jax.vmap(lambda a, i: a[i])(v, idx)
        return sparse_attend(qb, kg, vg, jnp.isfinite(vals))

    out = lax.map(block, (to_blocks(q), to_blocks(qi), to_blocks(wi), key_pos.reshape(nb, Q_BLOCK)))
    return out.swapaxes(0, 1).reshape(n, s, h, d)


def dsa_sample(q, k, v, qi, ki, wi, layer, cache_k, cache_v, cache_kidx, page_table):
    n, t, h, d = q.shape
    past = page_table.shape[1] * PAGE_SIZE
    n_phys = cache_k.shape[1]
    ktop = min(TOPK_MAX, (past + t) // 4)
    row_ids = layer * n_phys * PAGE_SIZE + (page_table[:, :, None] * PAGE_SIZE
                                            + jnp.arange(PAGE_SIZE)[None, None, :]).reshape(n, past)
    ki_all = jnp.concatenate([cache_kidx.reshape(-1, IDX_DIM)[row_ids].astype(ki.dtype), ki], axis=1)
    sc = index_scores(qi, ki_all, wi)
    qpos = past + jnp.arange(t)
    sc = jnp.where(jnp.arange(past + t)[None, None, :] <= qpos[None, :, None], sc, -jnp.inf)
    vals, idx = lax.top_k(sc, ktop)
    from_past = (idx < past)[..., None, None]
    phys = jax.vmap(lambda r, i: r[i])(row_ids, jnp.clip(idx, 0, past - 1))
    new_i = jnp.clip(idx - past, 0, t - 1)
    kg = jnp.where(from_past, cache_k.reshape(-1, h, d)[phys].astype(k.dtype), jax.vmap(lambda a, i: a[i])(k, new_i))
    vg = jnp.where(from_past, cache_v.reshape(-1, h, d)[phys].astype(v.dtype), jax.vmap(lambda a, i: a[i])(v, new_i))
    return sparse_attend(q, kg, vg, jnp.isfinite(vals))


def trunk_layer(x, pe, pos0, st_pool, st_conf, st_sc, attend,
                w_in, pool_mix, pool_scale, conv_w, conv_b, conv_ln_g, conv_ln_b, sconv_w,
                w_br_pool, w_br_conv, w_br_sconv, w_br_attn, w_o, ln1_g, ln1_b,
                w_ffn_in, w_ffn_out, w_pe, w_pg, ln2_g, ln2_b):
    n, t, _ = x.shape
    pos = pos0 + jnp.arange(t)
    g_cols, u_pool, u_conf, u_sc, q, k, v, qi, ki, wi = split_cols(x @ w_in)
    a_out, new_pool = pool_mixer(u_pool, st_pool, pos, pool_mix, pool_scale)
    ga, gb = jnp.split(u_conf, 2, axis=-1)
    cv, new_conf = causal_dwconv(ga * jax.nn.sigmoid(gb), st_conf, conv_w)
    b_out = jax.nn.silu(layer_norm(cv + conv_b, conv_ln_g, conv_ln_b))
    sb, sc_, sh = jnp.split(u_sc, 3, axis=-1)
    cc, new_sc = causal_dwconv(sc_ * sh, st_sc, sconv_w)
    c_out = sb * cc
    q = rope(q.reshape(n, t, N_HEADS, HEAD_DIM), pos)
    k = rope(k.reshape(n, t, N_HEADS, HEAD_DIM), pos)
    v = v.reshape(n, t, N_HEADS, HEAD_DIM)
    qi = rope(qi.reshape(n, t, IDX_HEADS, IDX_DIM), pos)
    ki = rope(ki[:, :, None, :], pos)[:, :, 0]
    d_out = attend(q, k, v, qi, ki, wi).reshape(n, t, ATT_W)
    gates = jax.nn.sigmoid(g_cols.reshape(n, t, N_BRANCH, D_MODEL))
    merged = (gates[:, :, 0] * (a_out @ w_br_pool) + gates[:, :, 1] * (b_out @ w_br_conv)
              + gates[:, :, 2] * (c_out @ w_br_sconv) + gates[:, :, 3] * (d_out @ w_br_attn))
    x1 = layer_norm(ALPHA * x + merged @ w_o, ln1_g, ln1_b)
    hg, hu = jnp.split(x1 @ w_ffn_in, 2, axis=-1)
    r = ALPHA * x1 + (jax.nn.silu(hg) * hu) @ w_ffn_out
    r = r + (pe @ w_pe) * jax.nn.sigmoid(r @ w_pg)
    y = layer_norm(r, ln2_g, ln2_b)
    return y, (k, v, ki), (new_pool, new_conf, new_sc)


def setup_inputs(seed: int = 0) -> dict:
    key = jax.random.key(seed)
    ks = jax.random.split(key, 40)
    f32 = jnp.float32

    def nrm(i, shape, scale=1.0):
        return jax.random.normal(ks[i], shape, f32) * scale

    n_pages = PAST_LEN // PAGE_SIZE
    n_used = DEC_BATCH * n_pages
    n_phys = n_used + n_used // 4
    page_table = jax.random.permutation(ks[8], n_phys)[:n_used].reshape(DEC_BATCH, n_pages).astype(jnp.int32)
    return {
        'x_prompt': nrm(0, (BATCH, SEQ, D_MODEL)),
        'x_sample': nrm(1, (DEC_BATCH, DEC_SEQ, D_MODEL)),
        'cache_k': nrm(2, (DEPTH, n_phys, PAGE_SIZE, N_HEADS, HEAD_DIM)),
        'cache_v': nrm(3, (DEPTH, n_phys, PAGE_SIZE, N_HEADS, HEAD_DIM)),
        'cache_kidx': nrm(4, (DEPTH, n_phys, PAGE_SIZE, IDX_DIM)),
        'state_pool': nrm(5, (DEPTH, DEC_BATCH, POOL_STATE, POOL_W)),
        'state_conv': nrm(6, (DEPTH, DEC_BATCH, CONF_K - 1, CONF_W)),
        'state_sconv': nrm(7, (DEPTH, DEC_BATCH, SCONV_K - 1, SCONV_W)),
        'page_table': page_table,
        'p_prompt': nrm(9, (DEPTH, BATCH, SEQ, PE_DIM)),
        'p_sample': nrm(10, (DEPTH, DEC_BATCH, DEC_SEQ, PE_DIM)),
        'w_in': nrm(11, (DEPTH, D_MODEL, IN_COLS), D_MODEL ** -0.5),
        'pool_mix': nrm(12, (DEPTH, POOL_GROUPS, POOL_GROUP_W, POOL_GROUP_W), POOL_GROUP_W ** -0.5),
        'pool_scale': 1.0 + nrm(13, (DEPTH, POOL_W), 0.02),
        'conv_w': nrm(14, (DEPTH, CONF_K, CONF_W), CONF_K ** -0.5),
        'conv_b': nrm(15, (DEPTH, CONF_W), 0.02),
        'conv_ln_g': 1.0 + nrm(16, (DEPTH, CONF_W), 0.02),
        'conv_ln_b': nrm(17, (DEPTH, CONF_W), 0.02),
        'sconv_w': nrm(18, (DEPTH, SCONV_K, SCONV_W), SCONV_K ** -0.5),
        'w_br_pool': nrm(19, (DEPTH, POOL_W, D_MODEL), POOL_W ** -0.5 * BETA),
        'w_br_conv': nrm(20, (DEPTH, CONF_W, D_MODEL), CONF_W ** -0.5 * BETA),
        'w_br_sconv': nrm(21, (DEPTH, SCONV_W, D_MODEL), SCONV_W ** -0.5 * BETA),
        'w_br_attn': nrm(22, (DEPTH, ATT_W, D_MODEL), ATT_W ** -0.5 * BETA),
        'w_o': nrm(23, (DEPTH, D_MODEL, D_MODEL), D_MODEL ** -0.5 * BETA),
        'ln1_g': 1.0 + nrm(24, (DEPTH, D_MODEL), 0.02),
        'ln1_b': nrm(25, (DEPTH, D_MODEL), 0.02),
        'w_ffn_in': nrm(26, (DEPTH, D_MODEL, 2 * D_FF), D_MODEL ** -0.5),
        'w_ffn_out': nrm(27, (DEPTH, D_FF, D_MODEL), D_FF ** -0.5 * BETA),
        'w_pe': nrm(28, (DEPTH, PE_DIM, D_MODEL), PE_DIM ** -0.5 * BETA),
        'w_pg': nrm(29, (DEPTH, D_MODEL, D_MODEL), D_MODEL ** -0.5),
        'ln2_g': 1.0 + nrm(30, (DEPTH, D_MODEL), 0.02),
        'ln2_b': nrm(31, (DEPTH, D_MODEL), 0.02),
    }


def reference(x_prompt, x_sample, cache_k, cache_v, cache_kidx, state_pool, state_conv, state_sconv,
              page_table, p_prompt, p_sample,
              w_in, pool_mix, pool_scale, conv_w, conv_b, conv_ln_g, conv_ln_b, sconv_w,
              w_br_pool, w_br_conv, w_br_sconv, w_br_attn, w_o, ln1_g, ln1_b,
              w_ffn_in, w_ffn_out, w_pe, w_pg, ln2_g, ln2_b):
    past = page_table.shape[1] * PAGE_SIZE
    n_b = x_prompt.shape[0]
    dt = x_prompt.dtype
    zero_pool = jnp.zeros((n_b, POOL_STATE, POOL_W), dt)
    zero_conf = jnp.zeros((n_b, CONF_K - 1, CONF_W), dt)
    zero_sc = jnp.zeros((n_b, SCONV_K - 1, SCONV_W), dt)
    y_p, y_s = x_prompt, x_sample
    kp, vp, kip, ks_, vs, kis = [], [], [], [], [], []
    pool_p, pool_s, conf_p, conf_s, sc_p, sc_s = [], [], [], [], [], []
    for i in range(DEPTH):
        lw = (w_in[i], pool_mix[i], pool_scale[i], conv_w[i], conv_b[i], conv_ln_g[i], conv_ln_b[i], sconv_w[i],
              w_br_pool[i], w_br_conv[i], w_br_sconv[i], w_br_attn[i], w_o[i], ln1_g[i], ln1_b[i],
              w_ffn_in[i], w_ffn_out[i], w_pe[i], w_pg[i], ln2_g[i], ln2_b[i])
        y_p, kv_p, st_p = trunk_layer(y_p, p_prompt[i], 0, zero_pool, zero_conf, zero_sc, dsa_prompt, *lw)
        attend_s = functools.partial(dsa_sample, layer=i, cache_k=cache_k, cache_v=cache_v,
                                     cache_kidx=cache_kidx, page_table=page_table)
        y_s, kv_s, st_s = trunk_layer(y_s, p_sample[i], past, state_pool[i], state_conv[i], state_sconv[i],
                                      attend_s, *lw)
        kp.append(kv_p[0]); vp.append(kv_p[1]); kip.append(kv_p[2])
        ks_.append(kv_s[0]); vs.append(kv_s[1]); kis.append(kv_s[2])
        pool_p.append(st_p[0]); conf_p.append(st_p[1]); sc_p.append(st_p[2])
        pool_s.append(st_s[0]); conf_s.append(st_s[1]); sc_s.append(st_s[2])
    return (y_p, y_s,
            jnp.stack(kp), jnp.stack(vp), jnp.stack(kip),
            jnp.stack(ks_), jnp.stack(vs), jnp.stack(kis),
            jnp.stack(pool_p), jnp.stack(pool_s),
            jnp.stack(conf_p), jnp.stack(conf_s),
            jnp.stack(sc_p), jnp.stack(sc_s))
```

```python
import numpy as np
import ml_dtypes
import concourse.bass as bass
import concourse.mybir as mybir
from concourse.bass_utils import run_bass_kernel_spmd

F32 = mybir.dt.float32
BF16 = mybir.dt.bfloat16
I32 = mybir.dt.int32
ALU = mybir.AluOpType
ACT = mybir.ActivationFunctionType
AX = mybir.AxisListType

D = 1024
KC = 8
DEPTH = 2
NCOLS = 7492
DFF = 2816
FC = 22
ALPHA = (2 * DEPTH) ** 0.25
EPS = 1e-5
PAGE = 128
TOPK = 256
C_G, C_POOL, C_CONF, C_SC, C_Q, C_K, C_V, C_QI, C_KI, C_WI = 0, 4096, 4352, 4864, 5632, 6144, 6656, 7168, 7424, 7488
E_QSW, E_KSW, E_QISW, E_KI2, E_KISW2 = 7492, 8004, 8516, 8772, 8900
NCOLS_EXT = 9028


class Op:
    __slots__ = ("eng", "fn", "deps", "dma", "sig", "sem", "val", "waits", "idx")


class Sched:
    ENGS = ("pe", "act", "dve", "pool", "sp")

    def __init__(self, nc):
        self.nc = nc
        self.ops = []
        self.last_w = {}
        self.readers = {}
        self.out_dmas = []

    def add(self, eng, fn, reads=(), writes=(), dma=False):
        op = Op()
        op.eng, op.fn, op.dma, op.sig, op.sem, op.val, op.waits = eng, fn, dma, False, None, 0, []
        xr = [k for k in reads if (k == "P" or (isinstance(k, tuple) and k[0] == "ps"))]
        if xr:
            writes = list(writes) + xr
        deps = set()
        for k in reads:
            w = self.last_w.get(k)
            if w is not None:
                deps.add(w)
        for k in writes:
            w = self.last_w.get(k)
            if w is not None:
                deps.add(w)
            r = self.readers.get(k)
            if r:
                deps.update(r)
        for k in reads:
            self.readers.setdefault(k, set()).add(op)
        for k in writes:
            self.last_w[k] = op
            self.readers[k] = set()
        deps.discard(op)
        op.deps = [d for d in deps if d.dma or dma or d.eng != eng or eng != "pe"]
        for d in op.deps:
            d.sig = True
        op.idx = len(self.ops)
        self.ops.append(op)
        return op

    def emit(self):
        nc = self.nc
        import contextlib
        with contextlib.ExitStack() as es:
            NDMA = 20
            dma_sems = {q: [es.enter_context(nc.semaphore(f"dq_{q}_{i}")) for i in range(NDMA)] for q in ("pool", "sp", "act")}
            dma_cnt = {q: [0] * NDMA for q in dma_sems}
            dma_rr = {q: 0 for q in dma_sems}
            eng_sems = {e: [es.enter_context(nc.semaphore(f"es_{e}_0"))] for e in self.ENGS}
            eng_cnt = {e: 0 for e in self.ENGS}
            SEM_CAP = 30000
            per_eng = {e: [] for e in self.ENGS}
            for op in self.ops:
                extra = []
                if op.dma:
                    q = op.eng
                    i = dma_rr[q]
                    dma_rr[q] = (i + 1) % NDMA
                    s = dma_sems[q][i]
                    if dma_cnt[q][i] > 0:
                        extra.append((s, dma_cnt[q][i] * 16))
                    dma_cnt[q][i] += 1
                    op.sem, op.val = s, dma_cnt[q][i] * 16
                elif op.sig:
                    e = op.eng
                    if eng_cnt[e] >= SEM_CAP:
                        eng_sems[e].append(es.enter_context(nc.semaphore(f"es_{e}_{len(eng_sems[e])}")))
                        eng_cnt[e] = 0
                    eng_cnt[e] += 1
                    op.sem, op.val = eng_sems[e][-1], eng_cnt[e]
                op.waits = extra + [(d.sem, d.val) for d in op.deps]
                per_eng[op.eng].append(op)
            final = [(op.sem, op.val) for op in self.out_dmas]
            waited = {e: {} for e in self.ENGS}

            def run(e_name, eng):
                wd = waited[e_name]
                for op in per_eng[e_name]:
                    for (s, v) in op.waits:
                        if wd.get(s, 0) < v:
                            eng.wait_ge(s, v)
                            wd[s] = v
                    ins = op.fn(eng)
                    if op.dma:
                        ins.then_inc(op.sem, 16)
                    elif op.sig:
                        ins.then_inc(op.sem, 1)
                if e_name == "sp":
                    for (s, v) in final:
                        if wd.get(s, 0) < v:
                            eng.wait_ge(s, v)
                            wd[s] = v

            with nc.Block() as block:
                block.tensor(lambda e: run("pe", e))
                block.scalar(lambda e: run("act", e))
                block.vector(lambda e: run("dve", e))
                block.gpsimd(lambda e: run("pool", e))
                block.sync(lambda e: run("sp", e))


class Cfg:
    def __init__(self, nps, S, nss, nphys, past=2048, T=4):
        self.nps, self.S, self.nss, self.nphys, self.past, self.T = nps, S, nss, nphys, past, T
        self.npages = past // PAGE
        self.sw = min(nss, 16)
        self.do_samples = True
        import os
        self.stop = int(os.environ.get('KSTOP', '0'))
        assert nss % self.sw == 0 and S % 512 == 0


def build_program(cfg):
    nc = bass.Bass("TRN2", target_bir_lowering=False)
    S, NPS, NSS, T = cfg.S, cfg.nps, cfg.nss, cfg.T
    NST = NSS * T
    sc = Sched(nc)
    import contextlib
    es = contextlib.ExitStack()

    def din(name, shape, dt=F32):
        return nc.dram_tensor(name, list(shape), dt, kind="ExternalInput").ap()

    def dout(name, shape, dt=F32):
        return nc.dram_tensor(name, list(shape), dt, kind="ExternalOutput").ap()

    xp = din("xp", [NPS * S, D]); xs = din("xs", [NST, D])
    pp = din("pp", [DEPTH, NPS * S, 256]); ps_in = din("ps", [DEPTH, NST, 256])
    ck = din("ck", [DEPTH * cfg.nphys * PAGE, 512]); cv = din("cv", [DEPTH * cfg.nphys * PAGE, 512])
    cki = din("cki", [DEPTH * cfg.nphys * PAGE, 64])
    st_pool = din("st_pool", [DEPTH, NSS * 15, 256]); st_conv = din("st_conv", [DEPTH, NSS * 30, 256])
    st_sc = din("st_sc", [DEPTH, NSS * 2, 256])
    pt = din("pt", [NSS, cfg.npages], I32)
    w_in = din("w_in", [DEPTH, D, NCOLS_EXT])
    pool_mix = din("pool_mix", [DEPTH, 2, 128, 128])
    vec256 = din("vec256", [DEPTH, 128, 2, 40])
    vec1024 = din("vec1024", [DEPTH, 128, 8, 4])
    w_brp = din("w_br_pool", [DEPTH, 256, D]); w_brc = din("w_br_conv", [DEPTH, 256, D])
    w_brs = din("w_br_sconv", [DEPTH, 256, D]); w_bra = din("w_br_attn", [DEPTH, 512, D])
    w_o = din("w_o", [DEPTH, D, D]); w_fi = din("w_ffn_in", [DEPTH, D, 2 * DFF]); w_fo = din("w_ffn_out", [DEPTH, DFF, D])
    w_pe = din("w_pe", [DEPTH, 256, D]); w_pg = din("w_pg", [DEPTH, D, D])
    c_ident = din("c_ident", [128, 128]); c_negm = din("c_negm", [128, 128])
    c_cos_p = din("c_cos_p", [128, S]); c_sin_p = din("c_sin_p", [128, S])
    c_cos_s = din("c_cos_s", [128, NST]); c_sin_s = din("c_sin_s", [128, NST])
    c_rcp0 = din("c_rcp0", [128, 2, 512]); c_rcp1 = din("c_rcp1", [128, 2, 512])
    c_iota = din("c_iota", [128, 1])
    c_smask = din("c_smask", [128, 128])

    yp = dout("yp", [NPS * S, D]); ys = dout("ys", [NST, D])
    nkp = dout("nkp", [DEPTH, NPS * S, 512]); nvp = dout("nvp", [DEPTH, NPS * S, 512]); nkip = dout("nkip", [DEPTH, NPS * S, 64])
    nks = dout("nks", [DEPTH, NST, 512]); nvs = dout("nvs", [DEPTH, NST, 512]); nkis = dout("nkis", [DEPTH, NST, 64])
    npool_p = dout("npool_p", [DEPTH, NPS * 15, 256]); npool_s = dout("npool_s", [DEPTH, NSS * 15, 256])
    nconv_p = dout("nconv_p", [DEPTH, NPS * 30, 256]); nconv_s = dout("nconv_s", [DEPTH, NSS * 30, 256])
    nsc_p = dout("nsc_p", [DEPTH, NPS * 2, 256]); nsc_s = dout("nsc_s", [DEPTH, NSS * 2, 256])
    act_p = nc.dram_tensor("act_p", [KC, 128, NPS * S], F32).ap()
    act_s = nc.dram_tensor("act_s", [KC, 128, NST], F32).ap()

    def sb(name, shape, dt=F32):
        return es.enter_context(nc.sbuf_tensor(name, list(shape), dt))

    W = 512
    XF = sb("XF", [128, KC, W]); XB = sb("XB", [128, KC, W], BF16)
    X1F = sb("X1F", [128, KC, W]); X1B = sb("X1B", [128, KC, W], BF16)
    NWB = 2
    WB = [sb(f"WB{i}", [128, 4096], BF16) for i in range(NWB)]
    ACTB = sb("ACTB", [128, FC, W], BF16)
    HMAX = max(1, cfg.sw)
    UP = sb("UP", [128, 2, max(15 + W, cfg.sw * 19)]); GL = sb("GL", [128, 2, max(30 + W, cfg.sw * 34)])
    SCX = sb("SCX", [128, 2, max(2 + W, cfg.sw * 6)]); SBG = sb("SBG", [128, 2, W])
    TMP = [sb(f"TMP{i}", [128, W]) for i in range(2)]
    QIT = sb("QIT", [128, 2, W])
    SK = max(S, cfg.past + 128)
    KT = sb("KT", [128, 4, S], BF16); VB = sb("VB", [128, max(S // 128, 6), 512], BF16); KIT = sb("KIT", [128, SK])
    IS = sb("IS", [128, SK]); JUNK = sb("JUNK", [128, SK], mybir.dt.uint8)
    PTB = sb("PTB", [128, cfg.sw * 16], I32); IDX = sb("IDX", [128, cfg.sw * 16], I32); SMK = sb("SMK", [128, 4])
    MKF = [sb(f"MKF{i}", [128, 128]) for i in range(2)]
    MKT = sb("MKT", [128, SK // 128, 128], BF16)
    EXPT = [sb(f"EXPT{i}", [128, 1024], BF16) for i in range(2)]
    STG = [sb(f"STG{i}", [128, 1024]) for i in range(2)]
    COS = sb("COS", [128, W]); SIN = sb("SIN", [128, W]); RCP = sb("RCP", [128, 2, 512])
    PET = sb("PET", [128, 2, W], BF16)
    IDENT = sb("IDENT", [128, 128]); NEGM = sb("NEGM", [128, 128]); ONES = sb("ONES", [128, 128]); ONESB = sb("ONESB", [128, 128], BF16)
    V256 = sb("V256", [128, 2, 40]); V1024 = sb("V1024", [128, KC, 4]); MIXB = sb("MIXB", [128, 2, 128], BF16)
    WQ = sb("WQ", [128, 4, 4]); SM = sb("SM", [128, 16])
    STAT = sb("STAT", [128, 2, W])
    HST = sb("HST", [128, 2, max(512, cfg.sw * 30)])

    CVB = sb("CVB", [128, 2, W]); CAUSN = sb("CAUSN", [128, 128], BF16); IDB = sb("IDB", [128, 128], BF16)
    WIB = sb("WIB", [128, KC, 4], BF16)
    c_causn = din("c_causn", [128, 128])
    NPS_BANKS = 8
    PSB = [es.enter_context(nc.psum_tensor(f"PS{i}", [128, 512], F32)) for i in range(NPS_BANKS)]
    ps_rr = [0]

    ps_n = [8]

    def psum():
        i = ps_rr[0] % ps_n[0]
        ps_rr[0] = (i + 1) % ps_n[0]
        return PSB[i], ("ps", i)

    wb_rr = [0]

    def wbuf():
        i = wb_rr[0]
        wb_rr[0] = (i + 1) % NWB
        return WB[i], ("wb", i)

    tmp_rr = [0]

    def tmpb():
        i = tmp_rr[0]
        tmp_rr[0] = (i + 1) % len(TMP)
        return TMP[i], ("tmp", i)

    stg_rr = [0]

    def stg():
        i = stg_rr[0]
        stg_rr[0] = (i + 1) % 2
        return STG[i], ("stg", i)

    dq_rr = [0]

    def dma(out, in_, reads, writes, q=None, final=False):
        if q is None:
            q = ("sp", "act")[dq_rr[0] % 2] if False else "sp"
        op = sc.add(q, lambda e, out=out, in_=in_: e.dma_start(out=out, in_=in_), reads, writes, dma=True)
        if final:
            sc.out_dmas.append(op)
        return op

    def cdma(out, in_, reads, writes):
        return sc.add("pool", lambda e, out=out, in_=in_: e.dma_start(out=out, in_=in_), reads, writes, dma=True)

    dma(IDENT[:, :], c_ident[:, :], [], ["ident"]); dma(NEGM[:, :], c_negm[:, :], [], ["negm"])
    sc.add("dve", lambda e: e.memset(ONES[:, :], 1.0), [], ["ones"])
    sc.add("dve", lambda e: e.memset(ONESB[:, :], 1.0), [], ["onesb"])
    import os
    SKIP = os.environ.get('KSKIP', '')
    if 'a' not in SKIP:
        cdma(CAUSN[:, :], c_causn[:, :], [], ["causn"]); cdma(IDB[:, :], c_ident[:, :], [], ["idb"])

    def linear(wview, col_lo, ncols, kcn, rhs_fn, Wt, consume, reads_rhs, mrows=None, dup=None):
        per = max(128, (4096 // kcn) // 128 * 128)
        c = 0
        while c < ncols:
            n = min(per, ncols - c)
            wb, wk = wbuf()
            wv = wb[:, 0:kcn * n].rearrange("p (k n) -> p k n", k=kcn)
            cdma(wv, wview[:, :, col_lo + c:col_lo + c + n], [], [wk])
            for j0 in range(0, n, 128):
                m = min(128, n - j0)
                ps, pk = psum()
                for kc in range(kcn):
                    sc.add("pe", lambda e, ps=ps, wv=wv, kc=kc, j0=j0, m=m, r=rhs_fn(kc), kcn=kcn:
                           e.matmul(ps[0:m, 0:Wt], wv[:, kc, j0:j0 + m], r, start=(kc == 0), stop=(kc == kcn - 1)),
                           [wk] + list(reads_rhs), [pk])
                consume((c + j0) // 128, ps, pk, m)
            c += n

    def wview_of(w2d, kcn):
        return w2d.rearrange("(k p) n -> p k n", p=128)

    def transpose_out(src_fn, nchunks, ntok, dst_dram_fn, src_keys, final=True, also=None):
        for t0 in range(0, ntok, 128):
            n = min(128, ntok - t0)
            ps, pk = psum()
            for c in range(nchunks):
                sc.add("pe", lambda e, ps=ps, c=c, t0=t0, n=n: e.transpose(ps[0:n, c * 128:(c + 1) * 128], src_fn(c)[:, t0:t0 + n], IDENT[:, :]),
                       list(src_keys) + ["ident"], [pk])
            st, sk = stg()
            sc.add("act", lambda e, st=st, ps=ps, n=n, nchunks=nchunks: e.copy(st[0:n, 0:nchunks * 128], ps[0:n, 0:nchunks * 128]), [pk], [sk])
            dma(dst_dram_fn(t0, n), st[0:n, 0:nchunks * 128], [sk], [], final=final)
            if also is not None:
                also(t0, n, st, sk)

    def lin_job(wview, kcn, specs, rhs_fn, Wt, reads_rhs, consume):
        wb, wk = wbuf()
        tot = sum(n for _, n in specs)
        wv = wb[:, 0:kcn * tot].rearrange("p (k n) -> p k n", k=kcn)
        o = 0
        for (c0, n) in specs:
            cdma(wv[:, :, o:o + n], wview[:, :, c0:c0 + n], [], [wk])
            o += n
        outs = []
        for j0 in range(0, tot, 128):
            m = min(128, tot - j0)
            ps, pk = psum()
            for kc in range(kcn):
                sc.add("pe", lambda e, ps=ps, wv=wv, kc=kc, j0=j0, m=m, r=rhs_fn(kc), last=(kc == kcn - 1):
                       e.matmul(ps[0:m, 0:Wt], wv[:, kc, j0:j0 + m], r, start=(kc == 0), stop=last),
                       [wk] + list(reads_rhs), [pk])
            outs.append((ps, pk, m))
        consume(outs)

    def layernorm(Xf, xkeys, nch, Wt, g_fn, b_fn, out_fn, okeys_fn, func=None):
        p1, k1 = psum(); p2, k2 = psum()
        n = float(nch * 128)
        for c in range(nch):
            sc.add("pe", lambda e, c=c: e.matmul(p1[:, 0:Wt], ONES[:, :], Xf[:, c, 0:Wt], start=(c == 0), stop=(c == nch - 1)),
                   [xkeys(c), "ones"], [k1])
        for c in range(nch):
            t, tk = tmpb()
            sc.add("act", lambda e, t=t, c=c: e.activation(t[:, 0:Wt], Xf[:, c, 0:Wt], ACT.Square), [xkeys(c)], [tk])
            sc.add("pe", lambda e, t=t, c=c: e.matmul(p2[:, 0:Wt], ONES[:, :], t[:, 0:Wt], start=(c == 0), stop=(c == nch - 1)),
                   [tk, "ones"], [k2])
        sc.add("dve", lambda e: e.tensor_scalar(STAT[:, 0, 0:Wt], p1[:, 0:Wt], 1.0 / n, None, ALU.mult), [k1], ["stat0"])
        t, tk = tmpb()
        sc.add("dve", lambda e, t=t: e.tensor_tensor(t[:, 0:Wt], STAT[:, 0, 0:Wt], STAT[:, 0, 0:Wt], ALU.mult), ["stat0"], [tk])
        sc.add("dve", lambda e, t=t: e.scalar_tensor_tensor(STAT[:, 1, 0:Wt], p2[:, 0:Wt], 1.0 / n, t[:, 0:Wt], ALU.mult, ALU.subtract),
               [k2, tk], ["stat1"])
        sc.add("dve", lambda e: e.tensor_scalar(STAT[:, 1, 0:Wt], STAT[:, 1, 0:Wt], EPS, None, ALU.add), ["stat1"], ["stat1"])
        sc.add("act", lambda e: e.activation(STAT[:, 1, 0:Wt], STAT[:, 1, 0:Wt], ACT.Sqrt), ["stat1"], ["stat1"])
        sc.add("dve", lambda e: e.reciprocal(STAT[:, 1, 0:Wt], STAT[:, 1, 0:Wt]), ["stat1"], ["stat1"])
        for c in range(nch):
            t, tk = tmpb()
            sc.add("dve", lambda e, t=t, c=c: e.tensor_tensor(t[:, 0:Wt], Xf[:, c, 0:Wt], STAT[:, 0, 0:Wt], ALU.subtract),
                   [xkeys(c), "stat0"], [tk])
            sc.add("dve", lambda e, t=t: e.tensor_tensor(t[:, 0:Wt], t[:, 0:Wt], STAT[:, 1, 0:Wt], ALU.mult), [tk, "stat1"], [tk])
            sc.add("act", lambda e, t=t, c=c: e.activation(out_fn(c), t[:, 0:Wt], (func or ACT.Identity), bias=b_fn(c), scale=g_fn(c)),
                   [tk, "v256", "v1024"], okeys_fn(c))

    def A_(c): return ACTB[:, c, :]
    MRG0, AO0, BO0, CO0, DO0, QT0 = 0, 8, 10, 12, 14, 18

    def akey(c): return ("actb", c)

    def run_tile(l, kind, seq, ti):
        if kind == "p":
            nseq, L, Wt = 1, 512, 512
            tok0 = seq * S + ti * 512
            x_in, p_in, a_scr = xp, pp, act_p
            pos0 = ti * 512
        else:
            nseq, L = cfg.sw, T
            Wt = nseq * T
            tok0 = ti * Wt
            x_in, p_in, a_scr = xs, ps_in, act_s
        last_layer = (l == DEPTH - 1)
        wv_in = w_in[l].rearrange("(k p) n -> p k n", p=128)

        if 'b' not in SKIP:
            dma(V256[:, :, :], vec256[l], [], ["v256"]); dma(V1024[:, :, :], vec1024[l], [], ["v1024"])
        if 'c' not in SKIP:
            cdma(MIXB[:, :, :], pool_mix[l].rearrange("c p n -> p c n"), [], ["mixb"])
        if 'd' in SKIP:
            return
        if 'h' in SKIP:
            pass
        elif kind == "p":
            dma(COS[:, 0:Wt], c_cos_p[:, pos0:pos0 + Wt], [], ["cos"]); dma(SIN[:, 0:Wt], c_sin_p[:, pos0:pos0 + Wt], [], ["sin"])
            dma(RCP[:, :, :], (c_rcp0 if ti == 0 else c_rcp1)[:, :, :], [], ["rcp"])
        else:
            dma(COS[:, 0:Wt], c_cos_s[:, tok0:tok0 + Wt], [], ["cos"]); dma(SIN[:, 0:Wt], c_sin_s[:, tok0:tok0 + Wt], [], ["sin"])
            dma(RCP[:, :, :], c_rcp1[:, :, :], [], ["rcp"])

        xk = lambda c: ("xf", c)
        xbk = lambda c: ("xb", c)
        if l == 0:
            for t0 in range(0, Wt, 128):
                n = min(128, Wt - t0)
                st, sk = stg()
                dma(st[0:n, :], x_in[tok0 + t0:tok0 + t0 + n, :], [], [sk])
                for half in range(2):
                    ps, pk = psum()
                    for c in range(4):
                        cc = half * 4 + c
                        if 'g' in SKIP:
                            continue
                        sc.add("pe", lambda e, ps=ps, c=c, cc=cc, st=st, n=n: e.transpose(ps[:, c * 128:c * 128 + n], st[0:n, cc * 128:(cc + 1) * 128], IDENT[0:n, 0:n]),
                               [sk, "ident"], [pk])
                    for c in range(4):
                        cc = half * 4 + c
                        if 'f' not in SKIP:
                            sc.add("act", lambda e, ps=ps, c=c, cc=cc, n=n, t0=t0: e.copy(XF[:, cc, t0:t0 + n], ps[:, c * 128:c * 128 + n]), [pk], [xk(cc)])
                        if 'e' not in SKIP:
                            sc.add("dve", lambda e, ps=ps, c=c, cc=cc, n=n, t0=t0: e.tensor_copy(XB[:, cc, t0:t0 + n], ps[:, c * 128:c * 128 + n]), [pk], [xbk(cc)])
        else:
            for c in range(KC):
                dma(XF[:, c, 0:Wt], a_scr[c, :, tok0:tok0 + Wt], [("ascr", kind, seq, ti)], [xk(c)])
                sc.add("dve", lambda e, c=c: e.tensor_copy(XB[:, c, 0:Wt], XF[:, c, 0:Wt]), [xk(c)], [xbk(c)])
        xb_all = [xbk(c) for c in range(KC)]
        rhs_x = lambda kc: XB[:, kc, 0:Wt]

        if cfg.stop == 1:
            return
        for t0 in range(0, Wt, 128):
            n = min(128, Wt - t0)
            st, sk = stg()
            dma(st[0:n, 0:256], p_in[l, tok0 + t0:tok0 + t0 + n, :], [], [sk])
            ps, pk = psum()
            for c in range(2):
                sc.add("pe", lambda e, ps=ps, c=c, st=st, n=n: e.transpose(ps[:, c * 128:c * 128 + n], st[0:n, c * 128:(c + 1) * 128], IDENT[0:n, 0:n]), [sk, "ident"], [pk])
            for c in range(2):
                sc.add("act", lambda e, ps=ps, c=c, n=n, t0=t0: e.copy(PET[:, c, t0:t0 + n], ps[:, c * 128:c * 128 + n]), [pk], [("pet", c)])

        def extv(buf, H):
            return buf[:, :, 0:nseq * (H + L)].rearrange("p c (s t) -> p c s t", s=nseq)
        UPv, GLv, SCv = extv(UP, 15), extv(GL, 30), extv(SCX, 2)
        halos = (("up", UP, UPv, 15, st_pool, npool_p, npool_s), ("gl", GL, GLv, 30, st_conv, nconv_p, nconv_s), ("scx", SCX, SCv, 2, st_sc, nsc_p, nsc_s))
        if kind == "p":
            if ti == 0:
                for (nm, buf, v, H, _, _, _) in halos:
                    sc.add("dve", lambda e, v=v, H=H: e.memset(v[:, :, 0, 0:H], 0.0), [], [nm])
            else:
                for (nm, buf, v, H, _, _, _) in halos:
                    sc.add("dve", lambda e, v=v, H=H: e.tensor_copy(v[:, :, 0, 0:H], v[:, :, 0, L:L + H]), [nm], [nm])
        else:
            for (nm, buf, v, H, st_in, _, _) in halos:
                nrow = nseq * H
                for r0 in range(0, nrow, 128):
                    n = min(128, nrow - r0)
                    st, sk = stg()
                    dma(st[0:n, 0:256], st_in[l, ti * nrow + r0:ti * nrow + r0 + n, :], [], [sk])
                    ps, pk = psum()
                    for c in range(2):
                        sc.add("pe", lambda e, ps=ps, c=c, st=st, n=n: e.transpose(ps[:, c * 128:c * 128 + n], st[0:n, c * 128:(c + 1) * 128], IDENT[0:n, 0:n]), [sk, "ident"], [pk])
                    for c in range(2):
                        sc.add("act", lambda e, ps=ps, c=c, n=n, r0=r0: e.copy(HST[:, c, r0:r0 + n], ps[:, c * 128:c * 128 + n]), [pk], ["hst"])
                for c in range(2):
                    sc.add("dve", lambda e, v=v, H=H, c=c, nrow=nrow: e.tensor_copy(v[:, c, :, 0:H], HST[:, c, 0:nrow].rearrange("p (s h) -> p s h", s=nseq)), ["hst"], [nm])

        def tokv(ap2d):
            return ap2d.rearrange("p (s t) -> p s t", s=nseq)

        if cfg.stop == 2:
            return
        def cons_copy_to(dst_fn, keys_fn, eng="act"):
            def f(outs, base=[0]):
                for (ps, pk, m) in outs:
                    c = base[0]; base[0] += 1
                    sc.add(eng, lambda e, ps=ps, c=c, m=m: e.copy(dst_fn(c), tokv(ps[0:m, 0:Wt])) if eng == "act" else e.tensor_copy(dst_fn(c), tokv(ps[0:m, 0:Wt])),
                           [pk], keys_fn(c))
            return f
        lin_job(wv_in, KC, [(C_POOL, 256)], rhs_x, Wt, xb_all, cons_copy_to(lambda c: UPv[:, c, :, 15:15 + L], lambda c: ["up"]))
        def cons_glu(outs):
            for c in range(2):
                (pa, ka, _), (pb, kb, _) = outs[c], outs[2 + c]
                t, tk = tmpb()
                sc.add("act", lambda e, t=t, pb=pb: e.activation(t[:, 0:Wt], pb[:, 0:Wt], ACT.Sigmoid), [kb], [tk])
                sc.add("dve", lambda e, t=t, pa=pa, c=c: e.tensor_tensor(GLv[:, c, :, 30:30 + L], tokv(pa[:, 0:Wt]), tokv(t[:, 0:Wt]), ALU.mult), [ka, tk], ["gl"])
        lin_job(wv_in, KC, [(C_CONF, 512)], rhs_x, Wt, xb_all, cons_glu)
        def cons_sb(outs):
            for c in range(2):
                ps, pk, _ = outs[c]
                sc.add("act", lambda e, ps=ps, c=c: e.copy(SBG[:, c, 0:Wt], ps[:, 0:Wt]), [pk], [("sbg", c)])
        lin_job(wv_in, KC, [(C_SC, 256)], rhs_x, Wt, xb_all, cons_sb)
        def cons_sch(outs):
            for c in range(2):
                (pa, ka, _), (pb, kb, _) = outs[c], outs[2 + c]
                t, tk = tmpb()
                sc.add("act", lambda e, t=t, pb=pb: e.copy(t[:, 0:Wt], pb[:, 0:Wt]), [kb], [tk])
                sc.add("dve", lambda e, t=t, pa=pa, c=c: e.tensor_tensor(SCv[:, c, :, 2:2 + L], tokv(pa[:, 0:Wt]), tokv(t[:, 0:Wt]), ALU.mult), [ka, tk], ["scx"])
        lin_job(wv_in, KC, [(C_SC + 256, 512)], rhs_x, Wt, xb_all, cons_sch)

        if cfg.stop == 3:
            return
        def rope_pair(col_plain, col_sw, ncol, dst_fn, keys_fn, scale=None):
            def cons(outs, nch=ncol // 128):
                for c in range(nch):
                    (pa, ka, _), (pb, kb, _) = outs[c], outs[nch + c]
                    t1, k1 = tmpb(); t2, k2 = tmpb()
                    sc.add("dve", lambda e, t1=t1, pa=pa: e.tensor_tensor(t1[:, 0:Wt], pa[:, 0:Wt], COS[:, 0:Wt], ALU.mult), [ka, "cos"], [k1])
                    sc.add("dve", lambda e, t2=t2, pb=pb: e.tensor_tensor(t2[:, 0:Wt], pb[:, 0:Wt], SIN[:, 0:Wt], ALU.mult), [kb, "sin"], [k2])
                    if scale is None:
                        sc.add("dve", lambda e, t1=t1, t2=t2, c=c: e.tensor_tensor(dst_fn(c), t1[:, 0:Wt], t2[:, 0:Wt], ALU.add), [k1, k2], keys_fn(c))
                    else:
                        sc.add("dve", lambda e, t1=t1, t2=t2, c=c: e.tensor_tensor(t1[:, 0:Wt], t1[:, 0:Wt], t2[:, 0:Wt], ALU.add), [k1, k2], [k1])
                        sc.add("act", lambda e, t1=t1, c=c: e.activation(dst_fn(c), t1[:, 0:Wt], ACT.Copy, scale=scale), [k1], keys_fn(c))
            for h0 in range(0, ncol, 256):
                nn = min(256, ncol - h0)
                lin_job(wv_in, KC, [(col_plain + h0, nn), (col_sw + h0, nn)], rhs_x, Wt, xb_all,
                        (lambda outs, h0=h0, nn=nn: _rp(outs, h0 // 128, nn // 128)))
        def make_rope(dst_fn, keys_fn, scale):
            def _rp(outs, cbase, nch):
                for c in range(nch):
                    (pa, ka, _), (pb, kb, _) = outs[c], outs[nch + c]
                    t1, k1 = tmpb(); t2, k2 = tmpb()
                    sc.add("dve", lambda e, t1=t1, pa=pa: e.tensor_tensor(t1[:, 0:Wt], pa[:, 0:Wt], COS[:, 0:Wt], ALU.mult), [ka, "cos"], [k1])
                    sc.add("dve", lambda e, t2=t2, pb=pb: e.tensor_tensor(t2[:, 0:Wt], pb[:, 0:Wt], SIN[:, 0:Wt], ALU.mult), [kb, "sin"], [k2])
                    cc = cbase + c
                    if scale is None:
                        sc.add("dve", lambda e, t1=t1, t2=t2, cc=cc: e.tensor_tensor(dst_fn(cc), t1[:, 0:Wt], t2[:, 0:Wt], ALU.add), [k1, k2], keys_fn(cc))
                    else:
                        sc.add("dve", lambda e, t1=t1, t2=t2: e.tensor_tensor(t1[:, 0:Wt], t1[:, 0:Wt], t2[:, 0:Wt], ALU.add), [k1, k2], [k1])
                        sc.add("dve", lambda e, cc=cc: e.memset(X1B[64:128, 2 * cc, 0:Wt], 0.0), [], [("x1b", 2 * cc)])
                        sc.add("dve", lambda e, cc=cc: e.memset(X1B[0:64, 2 * cc + 1, 0:Wt], 0.0), [], [("x1b", 2 * cc + 1)])
                        sc.add("act", lambda e, t1=t1, cc=cc: e.activation(X1B[0:64, 2 * cc, 0:Wt], t1[0:64, 0:Wt], ACT.Copy, scale=scale), [k1, ("x1b", 2 * cc)], [("x1b", 2 * cc)])
                        sc.add("act", lambda e, t1=t1, cc=cc: e.activation(X1B[64:128, 2 * cc + 1, 0:Wt], t1[64:128, 0:Wt], ACT.Copy, scale=scale), [k1, ("x1b", 2 * cc + 1)], [("x1b", 2 * cc + 1)])
            return _rp

        def rope_proj(col_plain, col_sw, ncol, dst_fn, keys_fn, scale=None):
            rp = make_rope(dst_fn, keys_fn, scale)
            for h0 in range(0, ncol, 256):
                nn = min(256, ncol - h0)
                lin_job(wv_in, KC, [(col_plain + h0, nn), (col_sw + h0, nn)], rhs_x, Wt, xb_all,
                        (lambda outs, h0=h0, nn=nn: rp(outs, h0 // 128, nn // 128)))

        rope_proj(C_Q, E_QSW, 512, None, None, scale=0.125)
        kfk = lambda c: ("x1f", c)
        rope_proj(C_K, E_KSW, 512, lambda c: X1F[:, c, 0:Wt], lambda c: [kfk(c)])
        lin_job(wv_in, KC, [(C_V, 512)], rhs_x, Wt, xb_all, cons_copy_to(lambda c: tokv(X1F[:, 4 + c, 0:Wt]), lambda c: [kfk(4 + c)]))
        rope_proj(C_QI, E_QISW, 256, lambda c: QIT[:, c, 0:Wt], lambda c: [("qit", c)])
        rope_proj(E_KI2, E_KISW2, 128, lambda c: HST[:, 1, 0:Wt], lambda c: ["kinew"])
        if cfg.stop == 4:
            return
        NQB = Wt // 128 if kind == "p" else 1
        cdma(WIB[:, :, :], wv_in[:, :, C_WI:C_WI + 4], [], ["wib"])
        for qb in range((Wt + 127) // 128):
            n = min(128, Wt - qb * 128)
            ps, pk = psum()
            for kc in range(KC):
                sc.add("pe", lambda e, ps=ps, kc=kc, qb=qb, n=n: e.matmul(ps[0:n, 0:4], XB[:, kc, qb * 128:qb * 128 + n], WIB[:, kc, :], start=(kc == 0), stop=(kc == KC - 1)),
                       [xbk(kc), "wib"], [pk])
            sc.add("act", lambda e, ps=ps, qb=qb, n=n: e.copy(WQ[0:n, qb, :], ps[0:n, 0:4]), [pk], ["wq"])

        nk_o, nv_o, nki_o = (nkp, nvp, nkip) if kind == "p" else (nks, nvs, nkis)
        if kind == "p":
            for c in range(4):
                sc.add("act", lambda e, c=c: e.copy(KT[:, c, pos0:pos0 + Wt], X1F[:, c, 0:Wt]), [kfk(c)], [("kt", c)])
            sc.add("act", lambda e: e.copy(KIT[:, pos0:pos0 + Wt], HST[:, 1, 0:Wt]), ["kinew"], ["kit"])
        transpose_out(lambda c: X1F[:, c, :], 4, Wt, lambda t0, n: nk_o[l, tok0 + t0:tok0 + t0 + n, :], [kfk(c) for c in range(4)])

        def v_also(t0, n, st, sk):
            if kind == "p":
                blk = (pos0 + t0) // 128
                sc.add("dve", lambda e, st=st, n=n, blk=blk: e.tensor_copy(VB[0:n, blk, :], st[0:n, 0:512]), [sk], [("vb", blk)])
        transpose_out(lambda c: X1F[:, 4 + c, :], 4, Wt, lambda t0, n: nv_o[l, tok0 + t0:tok0 + t0 + n, :], [kfk(4 + c) for c in range(4)], also=v_also)
        for t0 in range(0, Wt, 128):
            n = min(128, Wt - t0)
            ps, pk = psum()
            sc.add("pe", lambda e, ps=ps, t0=t0, n=n: e.transpose(ps[0:n, 0:64], HST[0:64, 1, t0:t0 + n], IDENT[0:64, 0:64]), ["kinew", "ident"], [pk])
            st, sk = stg()
            sc.add("act", lambda e, st=st, ps=ps, n=n: e.copy(st[0:n, 0:64], ps[0:n, 0:64]), [pk], [sk])
            dma(nki_o[l, tok0 + t0:tok0 + t0 + n, :], st[0:n, 0:64], [sk], [], final=True)

        if cfg.stop == 5:
            return
        EX = 15 + L
        NX = nseq * EX
        for c in range(2):
            e0 = UPv[:, c, :, :]
            def sv(i):
                return IS[:, i * NX:(i + 1) * NX].rearrange("p (s t) -> p s t", s=nseq)
            s2, s4, s8, s16 = sv(0), sv(1), sv(2), sv(3)
            sc.add("dve", lambda e, s2=s2, e0=e0: e.tensor_tensor(s2[:, :, 1:EX], e0[:, :, 1:EX], e0[:, :, 0:EX - 1], ALU.add), ["up"], ["is"])
            sc.add("dve", lambda e, s2=s2, s4=s4: e.tensor_tensor(s4[:, :, 3:EX], s2[:, :, 3:EX], s2[:, :, 1:EX - 2], ALU.add), ["is"], ["is"])
            if c == 1:
                sc.add("dve", lambda e, s8=s8, s4=s4: e.tensor_tensor(s8[:, :, 7:EX], s4[:, :, 7:EX], s4[:, :, 3:EX - 4], ALU.add), ["is"], ["is"])
                sc.add("dve", lambda e, s8=s8, s16=s16: e.tensor_tensor(s16[:, :, 15:EX], s8[:, :, 15:EX], s8[:, :, 7:EX - 8], ALU.add), ["is"], ["is"])
            lo_s, hi_s = (s2, s4) if c == 0 else (s8, s16)
            pooled = EXPT[0][:, c * 512:c * 512 + Wt]
            t, tk = tmpb()
            for (p0, ssrc) in ((0, lo_s), (64, hi_s)):
                sc.add("dve", lambda e, p0=p0, ssrc=ssrc, t=t, c=c: e.tensor_tensor(tokv(t[p0:p0 + 64, 0:Wt]), ssrc[p0:p0 + 64, :, 15:EX], tokv(RCP[p0:p0 + 64, c, 0:Wt]), ALU.mult),
                       ["is", "rcp"], [tk])
            sc.add("dve", lambda e, t=t, c=c, pooled=pooled: e.tensor_tensor(tokv(pooled), tokv(t[:, 0:Wt]), UPv[:, c, :, 15:EX], ALU.subtract), [tk, "up"], [("expt", 0)])
            ps, pk = psum()
            sc.add("pe", lambda e, ps=ps, c=c, pooled=pooled: e.matmul(ps[:, 0:Wt], MIXB[:, c, :], pooled, start=True, stop=True), [("expt", 0), "mixb"], [pk])
            sc.add("act", lambda e, ps=ps, c=c: e.activation(A_(AO0 + c)[:, 0:Wt], ps[:, 0:Wt], ACT.Copy, scale=V256[:, c, 37:38]), [pk, "v256"], [akey(AO0 + c)])

        if cfg.stop == 6:
            return
        for c in range(2):
            acc = tokv(CVB[:, c, 0:Wt])
            sc.add("dve", lambda e, acc=acc, c=c: e.tensor_scalar(acc, GLv[:, c, :, 0:L], V256[:, c, 0:1], None, ALU.mult), ["gl", "v256"], [("cvb", c)])
            for k in range(1, 31):
                sc.add("dve", lambda e, acc=acc, c=c, k=k: e.scalar_tensor_tensor(acc, GLv[:, c, :, k:k + L], V256[:, c, k:k + 1], acc, ALU.mult, ALU.add),
                       ["gl", "v256", ("cvb", c)], [("cvb", c)])
            sc.add("dve", lambda e, c=c: e.tensor_scalar(CVB[:, c, 0:Wt], CVB[:, c, 0:Wt], V256[:, c, 31:32], None, ALU.add), [("cvb", c), "v256"], [("cvb", c)])
        layernorm(CVB, lambda c: ("cvb", c), 2, Wt, lambda c: V256[:, c, 32:33], lambda c: V256[:, c, 33:34],
                  lambda c: A_(BO0 + c)[:, 0:Wt], lambda c: [akey(BO0 + c)], func=ACT.Silu)

        for c in range(2):
            t, tk = tmpb()
            acc = tokv(t[:, 0:Wt])
            sc.add("dve", lambda e, acc=acc, c=c: e.tensor_scalar(acc, SCv[:, c, :, 0:L], V256[:, c, 34:35], None, ALU.mult), ["scx", "v256"], [tk])
            for k in (1, 2):
                sc.add("dve", lambda e, acc=acc, c=c, k=k: e.scalar_tensor_tensor(acc, SCv[:, c, :, k:k + L], V256[:, c, 34 + k:35 + k], acc, ALU.mult, ALU.add), ["scx", "v256", tk], [tk])
            sc.add("dve", lambda e, t=t, c=c: e.tensor_tensor(A_(CO0 + c)[:, 0:Wt], t[:, 0:Wt], SBG[:, c, 0:Wt], ALU.mult), [tk, ("sbg", c)], [akey(CO0 + c)])

        if cfg.stop == 7:
            return
        last_tile = (kind == "s") or (ti == S // 512 - 1)
        if last_tile:
            for (nm, buf, v, H, _, o_p, o_s) in halos:
                nrow = nseq * H
                for c in range(2):
                    sc.add("dve", lambda e, v=v, H=H, c=c, nrow=nrow: e.tensor_copy(HST[:, c, 0:nrow].rearrange("p (s h) -> p s h", s=nseq), v[:, c, :, L:L + H]), [nm], ["hst"])
                o_d = o_p if kind == "p" else o_s
                rbase = seq * H if kind == "p" else ti * nrow
                for r0 in range(0, nrow, 128):
                    n = min(128, nrow - r0)
                    ps, pk = psum()
                    for c in range(2):
                        sc.add("pe", lambda e, ps=ps, c=c, r0=r0, n=n: e.transpose(ps[0:n, c * 128:(c + 1) * 128], HST[:, c, r0:r0 + n], IDENT[:, :]), ["hst", "ident"], [pk])
                    st, sk = stg()
                    sc.add("act", lambda e, st=st, ps=ps, n=n: e.copy(st[0:n, 0:256], ps[0:n, 0:256]), [pk], [sk])
                    dma(o_d[l, rbase + r0:rbase + r0 + n, :], st[0:n, 0:256], [sk], [], final=True)

        if cfg.stop == 8:
            return
        if kind == "p":
            ps_n[0] = 4
            ACC0, ACC1, DEN0, DEN1 = PSB[4], PSB[5], PSB[6], PSB[7]
            for qb in range(4):
                j = ti * 4 + qb
                nk = 128 * (j + 1)
                qs = slice(qb * 128, (qb + 1) * 128)
                use_idx = (j >= 2) and ('i' not in SKIP)
                if use_idx:
                    for h in range(4):
                        hp0 = (h % 2) * 64
                        for g0 in range(0, nk, 512):
                            ng = min(512, nk - g0)
                            ps, pk = psum()
                            sc.add("pe", lambda e, ps=ps, h=h, hp0=hp0, g0=g0, ng=ng, qs=qs: e.matmul(ps[:, 0:ng], QIT[hp0:hp0 + 64, h // 2, qs], KIT[hp0:hp0 + 64, g0:g0 + ng], start=True, stop=True),
                                   [("qit", h // 2), "kit"], [pk])
                            t, tk = tmpb()
                            sc.add("act", lambda e, t=t, ps=ps, ng=ng: e.activation(t[:, 0:ng], ps[:, 0:ng], ACT.Relu), [pk], [tk])
                            if h == 0:
                                sc.add("dve", lambda e, t=t, g0=g0, ng=ng, qb=qb: e.tensor_scalar(IS[:, g0:g0 + ng], t[:, 0:ng], WQ[:, qb, 0:1], None, ALU.mult), [tk, "wq"], ["is"])
                            else:
                                sc.add("dve", lambda e, t=t, g0=g0, ng=ng, qb=qb, h=h: e.scalar_tensor_tensor(IS[:, g0:g0 + ng], t[:, 0:ng], WQ[:, qb, h:h + 1], IS[:, g0:g0 + ng], ALU.mult, ALU.add), [tk, "wq", "is"], ["is"])
                    sc.add("dve", lambda e, nk=nk: e.tensor_tensor(IS[:, nk - 128:nk], IS[:, nk - 128:nk], NEGM[:, :], ALU.add), ["is", "negm"], ["is"])
                    sc.add("dve", lambda e, nk=nk: e.tensor_reduce(SM[:, 0:1], IS[:, 0:nk - 128], AX.X, ALU.min), ["is"], ["sm"])
                    sc.add("dve", lambda e, nk=nk: e.tensor_reduce(SM[:, 1:2], IS[:, 0:nk], AX.X, ALU.max), ["is"], ["sm"])
                    sc.add("dve", lambda e: e.tensor_tensor(SM[:, 2:3], SM[:, 1:2], SM[:, 0:1], ALU.subtract), ["sm"], ["sm"])
                    sc.add("dve", lambda e: e.tensor_scalar(SM[:, 2:3], SM[:, 2:3], 1.001, 1e-6, ALU.mult, ALU.add), ["sm"], ["sm"])
                    for it in range(26):
                        sc.add("dve", lambda e: e.tensor_scalar(SM[:, 2:3], SM[:, 2:3], 0.5, None, ALU.mult), ["sm"], ["sm"])
                        sc.add("dve", lambda e: e.tensor_tensor(SM[:, 3:4], SM[:, 0:1], SM[:, 2:3], ALU.add), ["sm"], ["sm"])
                        sc.add("dve", lambda e, nk=nk: e.tensor_scalar(JUNK[:, 0:nk], IS[:, 0:nk], SM[:, 3:4], 0.0, ALU.is_ge, ALU.add, accum_out=SM[:, 4:5]), ["is", "sm"], ["junk", "sm"])
                        sc.add("dve", lambda e: e.tensor_scalar(SM[:, 5:6], SM[:, 4:5], float(TOPK), SM[:, 2:3], ALU.is_ge, ALU.mult), ["sm"], ["sm"])
                        sc.add("dve", lambda e: e.tensor_tensor(SM[:, 0:1], SM[:, 0:1], SM[:, 5:6], ALU.add), ["sm"], ["sm"])
                    for c in range(j + 1):
                        mk = MKF[c % 2]
                        sc.add("dve", lambda e, mk=mk, c=c: e.tensor_scalar(mk[:, :], IS[:, c * 128:(c + 1) * 128], SM[:, 0:1], None, ALU.is_ge), ["is", "sm"], [("mkf", c % 2)])
                        ps, pk = psum()
                        sc.add("pe", lambda e, ps=ps, mk=mk: e.transpose(ps[:, 0:128], mk[:, :], IDENT[:, :]), [("mkf", c % 2), "ident"], [pk])
                        sc.add("act", lambda e, ps=ps, c=c: e.activation(MKT[:, c, :], ps[:, 0:128], ACT.Identity, bias=-30000.0, scale=30000.0), [pk], [("mkt", c)])
                for c in range(j + 1):
                    ex = EXPT[c % 2]
                    exk = ("expt", c % 2)
                    if 'q' in SKIP:
                        break
                    if use_idx:
                        mrhs, mkey = MKT[:, c, :], ("mkt", c)
                    elif c == j:
                        mrhs, mkey = CAUSN[:, :], "causn"
                    else:
                        mrhs, mkey = None, None
                    for hg in range(2):
                        ps, pk = psum()
                        for pp2 in range(2):
                            pair = hg * 2 + pp2
                            sc.add("pe", lambda e, ps=ps, pp2=pp2, pair=pair, c=c, qs=qs, last=(mrhs is None): e.matmul(ps[:, pp2 * 256:(pp2 + 1) * 256], KT[:, pair, c * 128:(c + 1) * 128], X1B[:, 2 * pair:2 * pair + 2, qs], start=True, stop=last),
                                   [("kt", pair), ("x1b", 2 * pair), ("x1b", 2 * pair + 1)], [pk])
                            if mrhs is not None:
                                for e2 in range(2):
                                    sc.add("pe", lambda e, ps=ps, pp2=pp2, e2=e2, mrhs=mrhs: e.matmul(ps[:, pp2 * 256 + e2 * 128:pp2 * 256 + (e2 + 1) * 128], IDB[:, :], mrhs, start=False, stop=(e2 == 1)), ["idb", mkey], [pk])
                        sc.add("act", lambda e, ps=ps, ex=ex, hg=hg: e.activation(ex[:, hg * 512:(hg + 1) * 512], ps[:, 0:512], ACT.Exp), [pk], [exk])
                    if 's' in SKIP:
                        continue
                    for h in range(8):
                        hp = h // 2
                        acc = ACC0 if h % 2 == 0 else ACC1
                        sc.add("pe", lambda e, acc=acc, hp=hp, h=h, c=c, ex=ex, j=j: e.matmul(acc[:, hp * 128:(hp + 1) * 128], VB[:, c, hp * 128:(hp + 1) * 128], ex[:, h * 128:(h + 1) * 128], start=(c == 0 and h < 2), stop=(c == j and h >= 6)),
                               [("vb", c), exk], [("ps", 4 + h % 2)])
                    for hg in range(2):
                        den = DEN0 if hg == 0 else DEN1
                        sc.add("pe", lambda e, den=den, hg=hg, ex=ex, c=c, j=j: e.matmul(den[:, 0:512], ONESB[:, :], ex[:, hg * 512:(hg + 1) * 512], start=(c == 0), stop=(c == j)), ["onesb", exk], [("ps", 6 + hg)])
                if 'q' in SKIP or 'r' in SKIP:
                    continue
                for hg in range(2):
                    den = DEN0 if hg == 0 else DEN1
                    sc.add("dve", lambda e, den=den, hg=hg: e.reciprocal(STAT[:, hg, 0:512], den[:, 0:512]), [("ps", 6 + hg)], [f"stat{hg}"])
                for h in range(8):
                    hp, hp0 = h // 2, (h % 2) * 64
                    acc = ACC0 if h % 2 == 0 else ACC1
                    sc.add("dve", lambda e, acc=acc, hp=hp, hp0=hp0, h=h, qs=qs: e.tensor_tensor(A_(DO0 + hp)[hp0:hp0 + 64, qs], acc[hp0:hp0 + 64, hp * 128:(hp + 1) * 128], STAT[hp0:hp0 + 64, h // 4, (h % 4) * 128:(h % 4 + 1) * 128], ALU.mult),
                           [("ps", 4 + h % 2), f"stat{h // 4}"], [akey(DO0 + hp)])
            ps_n[0] = 8
        else:
            sample_attention(l, ti, nseq, Wt, tok0)

        if cfg.stop == 9:
            return
        branches = ((w_brp, 2, AO0), (w_brc, 2, BO0), (w_brs, 2, CO0), (w_bra, 4, DO0))
        for b, (wbr, kb, a0) in enumerate(branches):
            wv_b = wbr[l].rearrange("(k p) n -> p k n", p=128)
            for m0 in range(0, 8, 2):
                holder = []
                lin_job(wv_in, KC, [(b * 1024 + m0 * 128, 256)], rhs_x, Wt, xb_all, lambda outs: holder.extend(outs))
                def cons_merge(outs, m0=m0, b=b, holder=holder):
                    for i in range(2):
                        m = m0 + i
                        (pg, kg, _), (pp_, kp, _) = holder[i], outs[i]
                        t, tk = tmpb()
                        sc.add("act", lambda e, t=t, pg=pg: e.activation(t[:, 0:Wt], pg[:, 0:Wt], ACT.Sigmoid), [kg], [tk])
                        if b == 0:
                            sc.add("dve", lambda e, t=t, pp_=pp_, m=m: e.tensor_tensor(X1F[:, m, 0:Wt], t[:, 0:Wt], pp_[:, 0:Wt], ALU.mult), [tk, kp], [("x1f", m)])
                        else:
                            sc.add("dve", lambda e, t=t, pp_=pp_: e.tensor_tensor(t[:, 0:Wt], t[:, 0:Wt], pp_[:, 0:Wt], ALU.mult), [tk, kp], [tk])
                            sc.add("dve", lambda e, t=t, m=m: e.tensor_tensor(X1F[:, m, 0:Wt], X1F[:, m, 0:Wt], t[:, 0:Wt], ALU.add), [tk, ("x1f", m)], [("x1f", m)])
                lin_job(wv_b, kb, [(m0 * 128, 256)], lambda kc, a0=a0: A_(a0 + kc)[:, 0:Wt], Wt, [akey(a0 + kc) for kc in range(kb)], cons_merge)
        for m in range(8):
            sc.add("act", lambda e, m=m: e.copy(A_(MRG0 + m)[:, 0:Wt], X1F[:, m, 0:Wt]), [("x1f", m)], [akey(MRG0 + m)])

        if cfg.stop == 10:
            return
        wv_o = w_o[l].rearrange("(k p) n -> p k n", p=128)
        for m0 in range(0, 8, 4):
            def cons_o(outs, m0=m0):
                for i, (ps, pk, _) in enumerate(outs):
                    m = m0 + i
                    sc.add("dve", lambda e, ps=ps, m=m: e.scalar_tensor_tensor(X1F[:, m, 0:Wt], XF[:, m, 0:Wt], ALPHA, ps[:, 0:Wt], ALU.mult, ALU.add), [pk, xk(m)], [("x1f", m)])
            lin_job(wv_o, KC, [(m0 * 128, 512)], lambda kc: A_(MRG0 + kc)[:, 0:Wt], Wt, [akey(MRG0 + kc) for kc in range(8)], cons_o)
        layernorm(X1F, lambda c: ("x1f", c), 8, Wt, lambda c: V1024[:, c, 0:1], lambda c: V1024[:, c, 1:2], lambda c: X1F[:, c, 0:Wt], lambda c: [("x1f", c)])
        for c in range(8):
            sc.add("act", lambda e, c=c: e.copy(X1B[:, c, 0:Wt], X1F[:, c, 0:Wt]), [("x1f", c)], [("x1b", c)])

        if cfg.stop == 11:
            return
        wv_fi = w_fi[l].rearrange("(k p) n -> p k n", p=128)
        x1b_all = [("x1b", c) for c in range(8)]
        for j0 in range(0, FC, 2):
            def cons_f(outs, j0=j0):
                for i in range(2):
                    (pg, kg, _), (pu, ku, _) = outs[i], outs[2 + i]
                    t, tk = tmpb()
                    sc.add("act", lambda e, t=t, pg=pg: e.activation(t[:, 0:Wt], pg[:, 0:Wt], ACT.Silu), [kg], [tk])
                    sc.add("dve", lambda e, t=t, pu=pu, j=j0 + i: e.tensor_tensor(ACTB[:, j, 0:Wt], t[:, 0:Wt], pu[:, 0:Wt], ALU.mult), [tk, ku], [akey(j0 + i)])
            lin_job(wv_fi, KC, [(j0 * 128, 256), (DFF + j0 * 128, 256)], lambda kc: X1B[:, kc, 0:Wt], Wt, x1b_all, cons_f)
        wv_fo = w_fo[l].rearrange("(k p) n -> p k n", p=128)
        for m in range(8):
            def cons_fo(outs, m=m):
                ps, pk, _ = outs[0]
                sc.add("dve", lambda e, ps=ps, m=m: e.scalar_tensor_tensor(XF[:, m, 0:Wt], X1F[:, m, 0:Wt], ALPHA, ps[:, 0:Wt], ALU.mult, ALU.add), [pk, ("x1f", m)], [xk(m)])
                sc.add("act", lambda e, m=m: e.copy(XB[:, m, 0:Wt], XF[:, m, 0:Wt]), [xk(m)], [xbk(m)])
            lin_job(wv_fo, FC, [(m * 128, 128)], lambda kc: ACTB[:, kc, 0:Wt], Wt, [akey(j) for j in range(FC)], cons_fo)

        if cfg.stop == 12:
            return
        wv_pe = w_pe[l].rearrange("(k p) n -> p k n", p=128)
        wv_pg = w_pg[l].rearrange("(k p) n -> p k n", p=128)
        for m0 in range(0, 8, 2):
            holder = []
            lin_job(wv_pe, 2, [(m0 * 128, 256)], lambda kc: PET[:, kc, 0:Wt], Wt, [("pet", 0), ("pet", 1)], lambda outs, holder=holder: holder.extend(outs))
            def cons_pg(outs, m0=m0, holder=holder):
                for i in range(2):
                    m = m0 + i
                    (pe_, ke, _), (pg, kg, _) = holder[i], outs[i]
                    t, tk = tmpb()
                    sc.add("act", lambda e, t=t, pg=pg: e.activation(t[:, 0:Wt], pg[:, 0:Wt], ACT.Sigmoid), [kg], [tk])
                    sc.add("dve", lambda e, t=t, pe_=pe_: e.tensor_tensor(t[:, 0:Wt], t[:, 0:Wt], pe_[:, 0:Wt], ALU.mult), [tk, ke], [tk])
                    sc.add("dve", lambda e, t=t, m=m: e.tensor_tensor(X1F[:, m, 0:Wt], XF[:, m, 0:Wt], t[:, 0:Wt], ALU.add), [tk, xk(m)], [("x1f", m)])
            lin_job(wv_pg, KC, [(m0 * 128, 256)], lambda kc: XB[:, kc, 0:Wt], Wt, [xbk(c) for c in range(8)], cons_pg)
        if cfg.stop == 13:
            return
        layernorm(X1F, lambda c: ("x1f", c), 8, Wt, lambda c: V1024[:, c, 2:3], lambda c: V1024[:, c, 3:4], lambda c: X1F[:, c, 0:Wt], lambda c: [("x1f", c)])
        if last_layer:
            y_o = yp if kind == "p" else ys
            for t0 in range(0, Wt, 128):
                n = min(128, Wt - t0)
                st, sk = stg()
                for half in range(2):
                    ps, pk = psum()
                    for c in range(4):
                        cc = half * 4 + c
                        sc.add("pe", lambda e, ps=ps, c=c, cc=cc, t0=t0, n=n: e.transpose(ps[0:n, c * 128:(c + 1) * 128], X1F[:, cc, t0:t0 + n], IDENT[:, :]), [("x1f", cc), "ident"], [pk])
                    sc.add("act", lambda e, st=st, ps=ps, n=n, half=half: e.copy(st[0:n, half * 512:(half + 1) * 512], ps[0:n, 0:512]), [pk], [sk])
                dma(y_o[tok0 + t0:tok0 + t0 + n, :], st[0:n, :], [sk], [], final=True)
        else:
            for c in range(KC):
                dma(a_scr[c, :, tok0:tok0 + Wt], X1F[:, c, 0:Wt], [("x1f", c)], [("ascr", kind, seq, ti)])

    def sample_attention(l, ti, nseq, Wt, tok0):
        NPG = cfg.npages
        PAST = cfg.past
        kfk = lambda c: ("x1f", c)
        NKS = PAST + T
        ps_n[0] = 4
        ACCB, DENB = PSB[4], PSB[5]
        ptv = pt[ti * nseq:(ti + 1) * nseq, :].rearrange("s g -> (s g)").partition_broadcast(128)
        dma(PTB[:, :], ptv, [], ["ptb"])
        PTF = GL[:, 1, 0:nseq * NPG]
        sc.add("dve", lambda e: e.tensor_copy(PTF, PTB[:, :]), ["ptb", "gl"], ["gl"])
        dma(SM[:, 8:9], c_iota[:, :], [], ["sm8"]); dma(SMK[:, :], c_smask[:, 0:4], [], ["smk"])
        sc.add("dve", lambda e: e.tensor_scalar(SM[:, 8:9], SM[:, 8:9], float(l * cfg.nphys * PAGE), None, ALU.add), ["sm8"], ["sm8"])
        sc.add("dve", lambda e: e.tensor_scalar(IDX[:, :], PTF, float(PAGE), SM[:, 8:9], ALU.mult, ALU.add), ["gl", "sm8"], ["idx"])
        sc.add("dve", lambda e: e.memset(IS[0:Wt, 0:NKS], 0.0), [], ["is"])

        def gather(out_ap, src, col, reads, writes):
            return sc.add("pool", lambda e, out_ap=out_ap, src=src, col=col: e.indirect_dma_start(out_ap, None, src, bass.IndirectOffsetOnAxis(IDX[:, col:col + 1], 0)),
                          ["idx"] + list(reads), writes, dma=True)

        KIP = [GL[:, 0, 0:128], GL[:, 0, 128:256]]
        for si in range(nseq):
            cs = slice(si * T, (si + 1) * T)
            for g in range(NPG):
                kp = KIP[g % 2]
                gather(kp[:, 0:64], cki[:, :], si * NPG + g, [], [("kip", g % 2)])
                sc.add("act", lambda e, kp=kp: e.copy(kp[:, 64:128], kp[:, 0:64]), [("kip", g % 2)], [("kip", g % 2)])
                ps, pk = psum()
                sc.add("pe", lambda e, ps=ps, kp=kp: e.transpose(ps[:, 0:128], kp, IDENT[:, :]), [("kip", g % 2), "ident"], [pk])
                sc.add("act", lambda e, ps=ps, g=g: e.copy(KIT[:, g * 128:(g + 1) * 128], ps[:, 0:128]), [pk], ["kit"])
            sc.add("act", lambda e, cs=cs: e.copy(KIT[:, PAST:PAST + T], HST[:, 1, cs]), ["kinew"], ["kit"])
            for c in range(2):
                sc.add("dve", lambda e, c=c: e.memset(CVB[:, c, 0:Wt], 0.0), [], [("cvb", c)])
                sc.add("dve", lambda e, c=c, cs=cs: e.tensor_copy(CVB[:, c, cs], QIT[:, c, cs]), [("qit", c)], [("cvb", c)])
            for h in range(4):
                hp0 = (h % 2) * 64
                for g0 in range(0, NKS, 512):
                    ng = min(512, NKS - g0)
                    ps, pk = psum()
                    sc.add("pe", lambda e, ps=ps, h=h, hp0=hp0, g0=g0, ng=ng: e.matmul(ps[0:Wt, 0:ng], CVB[hp0:hp0 + 64, h // 2, 0:Wt], KIT[hp0:hp0 + 64, g0:g0 + ng], start=True, stop=True),
                           [("cvb", h // 2), "kit"], [pk])
                    t, tk = tmpb()
                    sc.add("act", lambda e, t=t, ps=ps, ng=ng: e.activation(t[0:Wt, 0:ng], ps[0:Wt, 0:ng], ACT.Relu), [pk], [tk])
                    sc.add("dve", lambda e, t=t, g0=g0, ng=ng, h=h: e.scalar_tensor_tensor(IS[0:Wt, g0:g0 + ng], t[0:Wt, 0:ng], WQ[0:Wt, 0, h:h + 1], IS[0:Wt, g0:g0 + ng], ALU.mult, ALU.add), [tk, "wq", "is"], ["is"])
        sc.add("dve", lambda e: e.tensor_tensor(IS[0:Wt, PAST:PAST + T], IS[0:Wt, PAST:PAST + T], SMK[0:Wt, 0:T], ALU.add), ["is", "smk"], ["is"])
        nk = NKS
        sc.add("dve", lambda e: e.tensor_reduce(SM[0:Wt, 0:1], IS[0:Wt, 0:PAST], AX.X, ALU.min), ["is"], ["sm"])
        sc.add("dve", lambda e: e.tensor_reduce(SM[0:Wt, 1:2], IS[0:Wt, 0:nk], AX.X, ALU.max), ["is"], ["sm"])
        sc.add("dve", lambda e: e.tensor_tensor(SM[0:Wt, 2:3], SM[0:Wt, 1:2], SM[0:Wt, 0:1], ALU.subtract), ["sm"], ["sm"])
        sc.add("dve", lambda e: e.tensor_scalar(SM[0:Wt, 2:3], SM[0:Wt, 2:3], 1.001, 1e-6, ALU.mult, ALU.add), ["sm"], ["sm"])
        for it in range(26):
            sc.add("dve", lambda e: e.tensor_scalar(SM[0:Wt, 2:3], SM[0:Wt, 2:3], 0.5, None, ALU.mult), ["sm"], ["sm"])
            sc.add("dve", lambda e: e.tensor_tensor(SM[0:Wt, 3:4], SM[0:Wt, 0:1], SM[0:Wt, 2:3], ALU.add), ["sm"], ["sm"])
            sc.add("dve", lambda e: e.tensor_scalar(JUNK[0:Wt, 0:nk], IS[0:Wt, 0:nk], SM[0:Wt, 3:4], 0.0, ALU.is_ge, ALU.add, accum_out=SM[0:Wt, 4:5]), ["is", "sm"], ["junk", "sm"])
            sc.add("dve", lambda e: e.tensor_scalar(SM[0:Wt, 5:6], SM[0:Wt, 4:5], float(TOPK), SM[0:Wt, 2:3], ALU.is_ge, ALU.mult), ["sm"], ["sm"])
            sc.add("dve", lambda e: e.tensor_tensor(SM[0:Wt, 0:1], SM[0:Wt, 0:1], SM[0:Wt, 5:6], ALU.add), ["sm"], ["sm"])
        NCH = NPG + 1
        for c in range(NCH):
            n = 128 if c < NPG else T
            mk = MKF[c % 2]
            sc.add("dve", lambda e, mk=mk, c=c, n=n: e.tensor_scalar(mk[0:Wt, 0:n], IS[0:Wt, c * 128:c * 128 + n], SM[0:Wt, 0:1], None, ALU.is_ge), ["is", "sm"], [("mkf", c % 2)])
            ps, pk = psum()
            sc.add("pe", lambda e, ps=ps, mk=mk, n=n: e.transpose(ps[0:n, 0:Wt], mk[0:Wt, 0:n], IDENT[0:Wt, 0:Wt]), [("mkf", c % 2), "ident"], [pk])
            sc.add("act", lambda e, ps=ps, c=c, n=n: e.activation(MKT[0:n, c, 0:Wt], ps[0:n, 0:Wt], ACT.Identity, bias=-30000.0, scale=30000.0), [pk], [("mkt", c)])
        KTP = VB[:, 4, :].rearrange("p (a k) -> p a k", a=4)
        VN = VB[:, 5, :]
        for si in range(nseq):
            cs = slice(si * T, (si + 1) * T)
            ps, pk = psum()
            for c in range(4):
                sc.add("pe", lambda e, ps=ps, c=c, cs=cs: e.transpose(ps[0:T, c * 128:(c + 1) * 128], X1F[:, 4 + c, cs], IDENT[:, :]), [kfk(4 + c), "ident"], [pk])
            sc.add("act", lambda e, ps=ps: e.copy(VN[0:T, :], ps[0:T, 0:512]), [pk], [("vb", 5)])
            for g in range(NCH):
                n = 128 if g < NPG else T
                ex = EXPT[g % 2]
                exk = ("expt", g % 2)
                if g < NPG:
                    st, sk = stg()
                    gather(st[:, 0:512], ck[:, :], si * NPG + g, [], [sk])
                    ps, pk = psum()
                    for c in range(4):
                        sc.add("pe", lambda e, ps=ps, c=c, st=st: e.transpose(ps[:, c * 128:(c + 1) * 128], st[:, c * 128:(c + 1) * 128], IDENT[:, :]), [sk, "ident"], [pk])
                    sc.add("act", lambda e, ps=ps: e.copy(VB[:, 4, :], ps[:, 0:512]), [pk], [("vb", 4)])
                    vpg = VB[:, 2 + g % 2, :]
                    vkey = ("vb", 2 + g % 2)
                    gather(vpg, cv[:, :], si * NPG + g, [], [vkey])
                else:
                    for c in range(4):
                        sc.add("act", lambda e, c=c, cs=cs: e.copy(KTP[:, c, 0:T], X1F[:, c, cs]), [kfk(c)], [("vb", 4)])
                    vpg, vkey = VN, ("vb", 5)
                ps, pk = psum()
                for pair in range(4):
                    sc.add("pe", lambda e, ps=ps, pair=pair, n=n, cs=cs: e.matmul(ps[0:n, pair * 8:(pair + 1) * 8], KTP[:, pair, 0:n], X1B[:, 2 * pair:2 * pair + 2, cs], start=True, stop=False),
                           [("vb", 4), ("x1b", 2 * pair), ("x1b", 2 * pair + 1)], [pk])
                    for e2 in range(2):
                        sc.add("pe", lambda e, ps=ps, pair=pair, e2=e2, n=n, g=g, cs=cs: e.matmul(ps[0:n, pair * 8 + e2 * 4:pair * 8 + e2 * 4 + 4], IDB[0:n, 0:n], MKT[0:n, g, cs], start=False, stop=(e2 == 1)),
                               ["idb", ("mkt", g)], [pk])
                sc.add("act", lambda e, ps=ps, ex=ex, n=n: e.activation(ex[0:n, 0:32], ps[0:n, 0:32], ACT.Exp), [pk], [exk])
                for h in range(8):
                    sc.add("pe", lambda e, h=h, n=n, ex=ex, vpg=vpg, g=g: e.matmul(ACCB[0:T, h * 64:(h + 1) * 64], ex[0:n, h * 4:(h + 1) * 4], vpg[0:n, h * 64:(h + 1) * 64], start=(g == 0 and h == 0), stop=(g == NCH - 1 and h == 7)),
                           [exk, vkey], [("ps", 4)])
                for h in range(8):
                    sc.add("pe", lambda e, h=h, n=n, ex=ex, g=g: e.matmul(DENB[0:T, h:h + 1], ex[0:n, h * 4:(h + 1) * 4], ONESB[0:n, 0:1], start=(g == 0 and h == 0), stop=(g == NCH - 1 and h == 7)),
                           [exk, "onesb"], [("ps", 5)])
            sc.add("dve", lambda e: e.reciprocal(SM[0:T, 8:16], DENB[0:T, 0:8]), [("ps", 5), "sm8"], ["sm8"])
            t, tk = tmpb()
            for h in range(8):
                sc.add("dve", lambda e, t=t, h=h: e.tensor_scalar(t[0:T, h * 64:(h + 1) * 64], ACCB[0:T, h * 64:(h + 1) * 64], SM[0:T, 8 + h:9 + h], None, ALU.mult), [("ps", 4), "sm8"], [tk])
            ps, pk = psum()
            for c in range(4):
                sc.add("pe", lambda e, ps=ps, c=c, t=t: e.transpose(ps[:, c * T:(c + 1) * T], t[0:T, c * 128:(c + 1) * 128], IDENT[0:T, 0:T]), [tk, "ident"], [pk])
            for c in range(4):
                sc.add("act", lambda e, ps=ps, c=c, cs=cs: e.copy(A_(DO0 + c)[:, cs], ps[:, c * T:(c + 1) * T]), [pk], [akey(DO0 + c)])
        ps_n[0] = 8

    import os
    for l in range(int(os.environ.get('KL', DEPTH))):
        for seq in range(NPS):
            for ti in range(S // 512):
                run_tile(l, "p", seq, ti)
    print("sbuf remaining", nc.sbuf_bytes_remaining, "ops", len(sc.ops))
    if cfg.do_samples:
        for l in range(DEPTH):
            for ti in range(NSS // cfg.sw):
                run_tile(l, "s", 0, ti)
    sc.emit()
    return nc


def _swap_perm(col0, nheads):
    idx = []
    for h in range(nheads):
        for d in range(64):
            idx.append(col0 + h * 64 + (d + 32) % 64)
    return np.asarray(idx)


def _consts(S, nst, past, T):
    f32 = np.float32
    half = 32
    inv = (np.float32(10000.0) ** (-(np.arange(half, dtype=f32) / f32(half)))).astype(f32)
    rows_f = (np.arange(128) % 64) % 32
    sign = np.where((np.arange(128) % 64) < 32, -1.0, 1.0).astype(f32)

    def tables(pos):
        ang = (pos.astype(f32)[None, :] * inv[rows_f][:, None]).astype(f32)
        return np.cos(ang).astype(f32), (np.sin(ang).astype(f32) * sign[:, None]).astype(f32)
    cos_p, sin_p = tables(np.arange(S))
    cos_s, sin_s = tables(past + (np.arange(nst) % T))
    q = np.arange(128)
    negm = np.where(q[None, :] <= q[:, None], 0.0, -1e30).astype(f32)
    causn = np.where(q[:, None] <= q[None, :], 0.0, -30000.0).astype(f32)
    wins = np.asarray([2, 4, 8, 16], f32)
    g = 2 * np.arange(2)[None, :, None] + (np.arange(128)[:, None, None] >= 64)
    t = np.arange(512, dtype=f32)[None, None, :]
    rcp0 = (1.0 / np.minimum(wins[g], t + 1.0)).astype(f32)
    rcp1 = (1.0 / (wins[g] + 0.0 * t)).astype(f32)
    return dict(c_ident=np.eye(128, dtype=f32), c_negm=negm, c_causn=causn, c_cos_p=cos_p, c_sin_p=sin_p,
                c_cos_s=cos_s, c_sin_s=sin_s, c_rcp0=rcp0, c_rcp1=rcp1, c_iota=np.arange(128, dtype=f32)[:, None],
                c_smask=np.where(np.arange(128)[None, :] % 128 <= (np.arange(128)[:, None] % T), 0.0, -1e30).astype(f32))


def kernel(x_prompt, x_sample, cache_k, cache_v, cache_kidx, state_pool, state_conv, state_sconv,
           page_table, p_prompt, p_sample,
           w_in, pool_mix, pool_scale, conv_w, conv_b, conv_ln_g, conv_ln_b, sconv_w,
           w_br_pool, w_br_conv, w_br_sconv, w_br_attn, w_o, ln1_g, ln1_b,
           w_ffn_in, w_ffn_out, w_pe, w_pg, ln2_g, ln2_b, _ncores=None, _do_samples=True, _runner=None):
    f32 = np.float32
    A = lambda a: np.ascontiguousarray(np.asarray(a))
    B, S, _ = x_prompt.shape
    NB, T, _ = x_sample.shape
    if not _do_samples:
        cache_k, cache_v, cache_kidx = cache_k[:, :1], cache_v[:, :1], cache_kidx[:, :1]
    nphys = cache_k.shape[1]
    npages = page_table.shape[1]
    past = npages * PAGE
    ncores = _ncores or (8 if B >= 8 else 1)
    NPS, NSS = B // ncores, NB // ncores
    cfg = Cfg(NPS, S, NSS, nphys, past, T)
    cfg.do_samples = _do_samples
    nc = build_program(cfg)

    w_in = np.asarray(w_in)
    perm = np.concatenate([_swap_perm(C_Q, 8), _swap_perm(C_K, 8), _swap_perm(C_QI, 4)])
    ki = np.arange(C_KI, C_KI + 64)
    kisw = _swap_perm(C_KI, 1)
    w_in_ext = A(np.concatenate([w_in, w_in[:, :, perm], w_in[:, :, ki], w_in[:, :, ki], w_in[:, :, kisw], w_in[:, :, kisw]], axis=-1))
    assert w_in_ext.shape[-1] == NCOLS_EXT
    v256 = np.concatenate([np.asarray(conv_w), np.asarray(conv_b)[:, None], np.asarray(conv_ln_g)[:, None], np.asarray(conv_ln_b)[:, None],
                           np.asarray(sconv_w), np.asarray(pool_scale)[:, None], np.zeros((DEPTH, 2, 256), f32)], axis=1)
    v256 = A(v256.reshape(DEPTH, 40, 2, 128).transpose(0, 3, 2, 1))
    v1024 = np.stack([np.asarray(ln1_g), np.asarray(ln1_b), np.asarray(ln2_g), np.asarray(ln2_b)], axis=1)
    v1024 = A(v1024.reshape(DEPTH, 4, 8, 128).transpose(0, 3, 2, 1))
    pm = np.zeros((DEPTH, 2, 128, 128), f32)
    pmx = np.asarray(pool_mix)
    for c in range(2):
        pm[:, c, 0:64, 0:64] = pmx[:, 2 * c]
        pm[:, c, 64:128, 64:128] = pmx[:, 2 * c + 1]
    shared = dict(w_in=w_in_ext, pool_mix=pm, vec256=v256, vec1024=v1024,
                  w_br_pool=A(w_br_pool), w_br_conv=A(w_br_conv), w_br_sconv=A(w_br_sconv), w_br_attn=A(w_br_attn),
                  w_o=A(w_o), w_ffn_in=A(w_ffn_in), w_ffn_out=A(w_ffn_out), w_pe=A(w_pe), w_pg=A(w_pg),
                  ck=A(cache_k).reshape(-1, 512), cv=A(cache_v).reshape(-1, 512), cki=A(cache_kidx).reshape(-1, 64))
    shared.update(_consts(S, NSS * T, past, T))
    in_maps = []
    for r in range(ncores):
        ps_ = slice(r * NPS, (r + 1) * NPS)
        ss_ = slice(r * NSS, (r + 1) * NSS)
        m = dict(shared)
        m.update(xp=A(x_prompt[ps_]).reshape(NPS * S, D), xs=A(x_sample[ss_]).reshape(NSS * T, D),
                 pp=A(p_prompt[:, ps_]).reshape(DEPTH, NPS * S, 256), ps=A(p_sample[:, ss_]).reshape(DEPTH, NSS * T, 256),
                 st_pool=A(state_pool[:, ss_]).reshape(DEPTH, NSS * 15, 256), st_conv=A(state_conv[:, ss_]).reshape(DEPTH, NSS * 30, 256),
                 st_sc=A(state_sconv[:, ss_]).reshape(DEPTH, NSS * 2, 256), pt=A(page_table[ss_]).astype(np.int32))
        in_maps.append(m)
    if _runner is None:
        res = run_bass_kernel_spmd(nc, in_maps, core_ids=list(range(ncores)))
        R = res.results
    else:
        R = _runner(nc, in_maps)

    def cat(name, shape_fn, axis):
        return np.concatenate([shape_fn(np.asarray(R[r][name])) for r in range(ncores)], axis=axis).astype(f32)
    y_p = cat("yp", lambda a: a.reshape(NPS, S, D), 0)
    y_s = cat("ys", lambda a: a.reshape(NSS, T, D), 0)
    nkp = cat("nkp", lambda a: a.reshape(DEPTH, NPS, S, 8, 64), 1)
    nvp = cat("nvp", lambda a: a.reshape(DEPTH, NPS, S, 8, 64), 1)
    nkip = cat("nkip", lambda a: a.reshape(DEPTH, NPS, S, 64), 1)
    nks = cat("nks", lambda a: a.reshape(DEPTH, NSS, T, 8, 64), 1)
    nvs = cat("nvs", lambda a: a.reshape(DEPTH, NSS, T, 8, 64), 1)
    nkis = cat("nkis", lambda a: a.reshape(DEPTH, NSS, T, 64), 1)
    npp = cat("npool_p", lambda a: a.reshape(DEPTH, NPS, 15, 256), 1)
    nps_ = cat("npool_s", lambda a: a.reshape(DEPTH, NSS, 15, 256), 1)
    ncp = cat("nconv_p", lambda a: a.reshape(DEPTH, NPS, 30, 256), 1)
    ncs = cat("nconv_s", lambda a: a.reshape(DEPTH, NSS, 30, 256), 1)
    nsp = cat("nsc_p", lambda a: a.reshape(DEPTH, NPS, 2, 256), 1)
    nss_ = cat("nsc_s", lambda a: a.reshape(DEPTH, NSS, 2, 256), 1)
    return (y_p, y_s, nkp, nvp, nkip, nks, nvs, nkis, npp, nps_, ncp, ncs, nsp, nss_)
```
